# Optimizing a Trainium2 kernel written in Bass

```python
import math
import jax, jax.numpy as jnp
from jax import lax
import numpy as np

D_MODEL = 1024
BATCH = 16
SEQ = 256
DEPTH = 4
DEC_BATCH = 4
DEC_SEQ = 1024
PAST_LEN = 256

GRID_W = 64
Q_BLOCK = 128
HA = 4
DA = 64
HB = 8
KVB = 2
DB = 64
HC = 8
DC = 64
NA_ROWS = 8
NA_COLS = 16
D_FF = 4 * D_MODEL
N_BRANCH = 3
ROPE_BASE = 10000.0
EPS = 1e-6
NEG = -1e30

QA = 2 * HA * DA
KA = 2 * HA * DA
VA = HA * 2 * DA
QB = HB * DB
KB = KVB * DB
VB = KVB * DB
QC = HC * DC
KC = HC * DC
VC = HC * DC
D_IN = QA + KA + VA + QB + KB + VB + QC + KC + VC + N_BRANCH * D_MODEL
SPLIT_IDX = (QA, QA + KA, QA + KA + VA, QA + KA + VA + QB, QA + KA + VA + QB + KB,
             QA + KA + VA + QB + KB + VB, QA + KA + VA + QB + KB + VB + QC,
             QA + KA + VA + QB + KB + VB + QC + KC, QA + KA + VA + QB + KB + VB + QC + KC + VC)
OUT_A = HA * 2 * DA
OUT_B = HB * DB
OUT_C = HC * DC

kernel_name = "hybrid_diffusion_parallel_mixer_step"


def rmsnorm(x, g):
    xf = x.astype(jnp.float32)
    y = xf * lax.rsqrt(jnp.mean(xf * xf, axis=-1, keepdims=True) + EPS)
    return (y * g.astype(jnp.float32)).astype(x.dtype)


def axial_rope_tables(S, head_dim):
    nf = head_dim // 4
    t = jnp.arange(S)
    row = (t // GRID_W).astype(jnp.float32)
    col = (t % GRID_W).astype(jnp.float32)
    inv = ROPE_BASE ** (-jnp.arange(nf, dtype=jnp.float32) / nf)
    ar = row[:, None] * inv[None, :]
    ac = col[:, None] * inv[None, :]
    return jnp.cos(ar), jnp.sin(ar), jnp.cos(ac), jnp.sin(ac)


def _rope_half(x, cos, sin):
    x1, x2 = jnp.split(x, 2, axis=-1)
    c = cos[:, None, :]
    s = sin[:, None, :]
    return jnp.concatenate([x1 * c - x2 * s, x2 * c + x1 * s], axis=-1)


def apply_axial_rope(x, rope):
    cr, sr, cc, sc = rope
    xr, xc = jnp.split(x, 2, axis=-1)
    return jnp.concatenate([_rope_half(xr, cr, sr), _rope_half(xc, cc, sc)], axis=-1).astype(x.dtype)


def blocked_attention(q, k, v):
    B, Sq, Hkv, G, D = q.shape
    nb = Sq // Q_BLOCK
    scale = D ** -0.5
    qb = jnp.moveaxis(q.reshape(B, nb, Q_BLOCK, Hkv, G, D), 1, 0)

    def one_block(qblk):
        s = jnp.einsum('bqhgd,bkhd->bhgqk', qblk, k, preferred_element_type=jnp.float32) * scale
        p = jax.nn.softmax(s, axis=-1).astype(v.dtype)
        return jnp.einsum('bhgqk,bkhe->bqhge', p, v)

    out = lax.map(one_block, qb)
    return jnp.moveaxis(out, 0, 1).reshape(B, Sq, Hkv, G, v.shape[-1])


def diff_lambda(lambda_qk_l, lam_init):
    lq = lambda_qk_l.astype(jnp.float32)
    return jnp.exp(jnp.sum(lq[0] * lq[1])) - jnp.exp(jnp.sum(lq[2] * lq[3])) + lam_init


def diff_attention(q, k, v, lam, lam_init, g_subln):
    B, Sq = q.shape[:2]
    o1 = blocked_attention(q[:, :, :HA, None], k[:, :, :HA], v)[:, :, :, 0]
    o2 = blocked_attention(q[:, :, HA:, None], k[:, :, HA:], v)[:, :, :, 0]
    o = o1 - lam.astype(o1.dtype) * o2
    o = rmsnorm(o, g_subln) * (1.0 - lam_init)
    return o.reshape(B, Sq, OUT_A)


def gqa_attention(q, k, v):
    B, Sq = q.shape[:2]
    o = blocked_attention(q.reshape(B, Sq, KVB, HB // KVB, DB), k, v)
    return o.reshape(B, Sq, OUT_B)


def neighbourhood_attention(q, k, v, k_ctx, v_ctx, rel_bias):
    B, S, H, D = q.shape
    rows = S // GRID_W
    wr = min(NA_ROWS, rows)
    scale = D ** -0.5
    r = jnp.arange(rows)
    r0 = jnp.clip(r - wr // 2, 0, rows - wr)
    key_rows = r0[:, None] + jnp.arange(wr)[None, :]
    kg = k.reshape(B, rows, GRID_W, H, D)[:, key_rows].reshape(B, rows, wr * GRID_W, H, D)
    vg = v.reshape(B, rows, GRID_W, H, D)[:, key_rows].reshape(B, rows, wr * GRID_W, H, D)
    qg = q.reshape(B, rows, GRID_W, H, D)
    cq = jnp.arange(GRID_W)
    c0 = jnp.clip(cq - NA_COLS // 2, 0, GRID_W - NA_COLS)
    ck = jnp.arange(GRID_W)
    in_win = (ck[None, :] >= c0[:, None]) & (ck[None, :] < c0[:, None] + NA_COLS)
    dr_idx = key_rows - r[:, None] + (NA_ROWS - 1)
    dc_idx = jnp.clip(ck[None, :] - cq[:, None] + (NA_COLS - 1), 0, 2 * NA_COLS - 2)
    bias = rel_bias[:, dr_idx[:, None, :, None], dc_idx[None, :, None, :]]
    bias = jnp.where(in_win[None, None, :, None, :], bias.astype(jnp.float32), NEG)
    bias = bias.reshape(H, rows, GRID_W, wr * GRID_W)
    s_lat = jnp.einsum('brqhd,brkhd->bhrqk', qg, kg, preferred_element_type=jnp.float32) * scale + bias[None]
    s_ctx = jnp.einsum('brqhd,bkhd->bhrqk', qg, k_ctx, preferred_element_type=jnp.float32) * scale
    p = jax.nn.softmax(jnp.concatenate([s_lat, s_ctx], axis=-1), axis=-1).astype(v.dtype)
    p_lat, p_ctx = p[..., :wr * GRID_W], p[..., wr * GRID_W:]
    o = jnp.einsum('bhrqk,brkhd->brqhd', p_lat, vg) + jnp.einsum('bhrqk,bkhd->brqhd', p_ctx, v_ctx)
    return o.reshape(B, S, OUT_C)


def modulation(cvec, w_mod_l, b_mod_l):
    m = jax.nn.silu(cvec) @ w_mod_l + b_mod_l
    return [t[:, None, :] for t in jnp.split(m, 6, axis=-1)]


def project(h, w_in_l, b_gate_l):
    B, S = h.shape[:2]
    z = h @ w_in_l
    qa, ka, va, qb, kb, vb, qc, kc, vc, g = jnp.split(z, SPLIT_IDX, axis=-1)
    gates = jax.nn.sigmoid((g + b_gate_l).astype(jnp.float32)).astype(h.dtype).reshape(B, S, N_BRANCH, D_MODEL)
    return (qa.reshape(B, S, 2 * HA, DA), ka.reshape(B, S, 2 * HA, DA), va.reshape(B, S, HA, 2 * DA),
            qb.reshape(B, S, HB, DB), kb.reshape(B, S, KVB, DB), vb.reshape(B, S, KVB, DB),
            qc.reshape(B, S, HC, DC), kc.reshape(B, S, HC, DC), vc.reshape(B, S, HC, DC), gates)


def merge(oa, ob, oc, gates, w_ba, w_bb, w_bc, w_o):
    y = gates[:, :, 0] * (oa @ w_ba) + gates[:, :, 1] * (ob @ w_bb) + gates[:, :, 2] * (oc @ w_bc)
    return y @ w_o


def sq_relu_mlp(h, w1, w2):
    return jnp.square(jax.nn.relu(h @ w1)) @ w2


def setup_inputs(seed: int = 0) -> dict:
    key = jax.random.key(seed)
    ks = jax.random.split(key, 32)
    f32 = jnp.float32

    def nrm(k, shape, scale):
        return jax.random.normal(k, shape, f32) * scale

    L = PAST_LEN
    return {
        "x_prompt": nrm(ks[0], (BATCH, SEQ, D_MODEL), 1.0),
        "x_sample": nrm(ks[1], (DEC_BATCH, DEC_SEQ, D_MODEL), 1.0),
        "c": nrm(ks[2], (DEC_BATCH, D_MODEL), 1.0),
        "cache_a_k": nrm(ks[3], (DEC_BATCH, DEPTH, L, 2 * HA, DA), 1.0),
        "cache_a_v": nrm(ks[4], (DEC_BATCH, DEPTH, L, HA, 2 * DA), 1.0),
        "cache_b_k": nrm(ks[5], (DEC_BATCH, DEPTH, L, KVB, DB), 1.0),
        "cache_b_v": nrm(ks[6], (DEC_BATCH, DEPTH, L, KVB, DB), 1.0),
        "cache_c_k": nrm(ks[7], (DEC_BATCH, DEPTH, L, HC, DC), 1.0),
        "cache_c_v": nrm(ks[8], (DEC_BATCH, DEPTH, L, HC, DC), 1.0),
        "c_ctx": nrm(ks[9], (D_MODEL,), 1.0),
        "w_mod": nrm(ks[10], (DEPTH, D_MODEL, 6 * D_MODEL), D_MODEL ** -0.5),
        "b_mod": nrm(ks[11], (DEPTH, 6 * D_MODEL), 0.02),
        "g_norm1": 1.0 + nrm(ks[12], (DEPTH, D_MODEL), 0.1),
        "g_norm2": 1.0 + nrm(ks[13], (DEPTH, D_MODEL), 0.1),
        "w_in": nrm(ks[14], (DEPTH, D_MODEL, D_IN), D_MODEL ** -0.5),
        "b_gate": nrm(ks[15], (DEPTH, N_BRANCH * D_MODEL), 0.02),
        "lambda_qk": nrm(ks[16], (DEPTH, 4, DA), 0.1),
        "g_subln": 1.0 + nrm(ks[17], (DEPTH, 2 * DA), 0.1),
        "g_qnorm": 1.0 + nrm(ks[18], (DEPTH, DB), 0.1),
        "g_knorm": 1.0 + nrm(ks[19], (DEPTH, DB), 0.1),
        "rel_bias": nrm(ks[20], (DEPTH, HC, 2 * NA_ROWS - 1, 2 * NA_COLS - 1), 0.1),
        "w_branch_a": nrm(ks[21], (DEPTH, OUT_A, D_MODEL), OUT_A ** -0.5),
        "w_branch_b": nrm(ks[22], (DEPTH, OUT_B, D_MODEL), OUT_B ** -0.5),
        "w_branch_c": nrm(ks[23], (DEPTH, OUT_C, D_MODEL), OUT_C ** -0.5),
        "w_out": nrm(ks[24], (DEPTH, D_MODEL, D_MODEL), D_MODEL ** -0.5),
        "w_ff1": nrm(ks[25], (DEPTH, D_MODEL, D_FF), D_MODEL ** -0.5),
        "w_ff2": nrm(ks[26], (DEPTH, D_FF, D_MODEL), D_FF ** -0.5),
        "g_final": 1.0 + nrm(ks[27], (D_MODEL,), 0.1),
    }


def reference(x_prompt, x_sample, c, cache_a_k, cache_a_v, cache_b_k, cache_b_v, cache_c_k, cache_c_v,
              c_ctx, w_mod, b_mod, g_norm1, g_norm2, w_in, b_gate, lambda_qk, g_subln, g_qnorm, g_knorm,
              rel_bias, w_branch_a, w_branch_b, w_branch_c, w_out, w_ff1, w_ff2, g_final):
    xp = x_prompt
    xs = x_sample
    S = xs.shape[1]
    rope = axial_rope_tables(S, DA)
    new_ak, new_av, new_bk, new_bv, new_ck, new_cv = [], [], [], [], [], []
    for l in range(DEPTH):
        lam_init = 0.8 - 0.6 * math.exp(-0.3 * l)
        lam = diff_lambda(lambda_qk[l], lam_init)
        mp = modulation(c_ctx[None, :], w_mod[l], b_mod[l])
        ms = modulation(c, w_mod[l], b_mod[l])

        h = rmsnorm(xp, g_norm1[l]) * (1.0 + mp[1]) + mp[0]
        qa, ka, va, qb, kb, vb, qc, kc, vc, gates = project(h, w_in[l], b_gate[l])
        qb = rmsnorm(qb, g_qnorm[l])
        kb = rmsnorm(kb, g_knorm[l])
        oa = diff_attention(qa, ka, va, lam, lam_init, g_subln[l])
        ob = gqa_attention(qb, kb, vb)
        oc = blocked_attention(qc[:, :, :, None], kc, vc).reshape(xp.shape[0], xp.shape[1], OUT_C)
        xp = xp + mp[2] * merge(oa, ob, oc, gates, w_branch_a[l], w_branch_b[l], w_branch_c[l], w_out[l])
        h2 = rmsnorm(xp, g_norm2[l]) * (1.0 + mp[4]) + mp[3]
        xp = xp + mp[5] * sq_relu_mlp(h2, w_ff1[l], w_ff2[l])
        new_ak.append(ka)
        new_av.append(va)
        new_bk.append(kb)
        new_bv.append(vb)
        new_ck.append(kc)
        new_cv.append(vc)

        h = rmsnorm(xs, g_norm1[l]) * (1.0 + ms[1]) + ms[0]
        qa, ka, va, qb, kb, vb, qc, kc, vc, gates = project(h, w_in[l], b_gate[l])
        qa = apply_axial_rope(qa, rope)
        ka = apply_axial_rope(ka, rope)
        qb = apply_axial_rope(rmsnorm(qb, g_qnorm[l]), rope)
        kb = apply_axial_rope(rmsnorm(kb, g_knorm[l]), rope)
        oa = diff_attention(qa, jnp.concatenate([ka, cache_a_k[:, l]], axis=1),
                            jnp.concatenate([va, cache_a_v[:, l]], axis=1), lam, lam_init, g_subln[l])
        ob = gqa_attention(qb, jnp.concatenate([kb, cache_b_k[:, l]], axis=1),
                           jnp.concatenate([vb, cache_b_v[:, l]], axis=1))
        oc = neighbourhood_attention(qc, kc, vc, cache_c_k[:, l], cache_c_v[:, l], rel_bias[l])
        xs = xs + ms[2] * merge(oa, ob, oc, gates, w_branch_a[l], w_branch_b[l], w_branch_c[l], w_out[l])
        h2 = rmsnorm(xs, g_norm2[l]) * (1.0 + ms[4]) + ms[3]
        xs = xs + ms[5] * sq_relu_mlp(h2, w_ff1[l], w_ff2[l])

    y_prompt = rmsnorm(xp, g_final)
    y_sample = rmsnorm(xs, g_final)
    new_a_k = jnp.stack(new_ak, axis=1)
    new_a_v = jnp.stack(new_av, axis=1)
    new_b_k = jnp.stack(new_bk, axis=1)
    new_b_v = jnp.stack(new_bv, axis=1)
    new_c_k = jnp.stack(new_ck, axis=1)
    new_c_v = jnp.stack(new_cv, axis=1)
    return (y_prompt, y_sample, new_a_k, new_a_v, new_b_k, new_b_v, new_c_k, new_c_v)
```

```python
import math
from contextlib import ExitStack

import numpy as np
import concourse.bass as bass
import concourse.mybir as mybir
from concourse.bass_utils import run_bass_kernel_spmd

F32 = mybir.dt.float32
BF16 = mybir.dt.bfloat16
AF = mybir.ActivationFunctionType
ALU = mybir.AluOpType
AX = mybir.AxisListType

D = 1024
T = 1024
DEPTH = 4
NCORE = 8
DIN = 6912
DFF = 4096
EPS = 1e-6
NEGM = -30000.0
SAME_ENG_SYNC = True
SAME_ENG_DIST = 10 ** 9
NSLOT = 3
WSLOT = 4096

C_QA, C_KA, C_VA, C_QB, C_KB, C_VB, C_QC, C_KC, C_VC, C_G = 0, 512, 1024, 1536, 2048, 2176, 2304, 2816, 3328, 3840

_SM = {}
_off = 0
for _n, _w in (("cvec", 8), ("bmod", 4 * 48), ("gn1", 32), ("gn2", 32), ("gfin", 8), ("bgate", 4 * 24),
               ("gsub", 4), ("gq", 4), ("gk", 4), ("mb", 20), ("fm", 80)):
    _SM[_n] = (_off, _w)
    _off += _w
NSM = _off

C_KT = {0: [0, 1, 2, 3], 1: [0, 1, 2, 3, 4, 5], 2: [2, 3, 4, 5, 6, 7], 3: [4, 5, 6, 7]}
def _f_allones(kt, qs):
    for rqi in range(4):
        rq = 4 * qs + rqi
        r0 = min(max(rq - 4, 0), 8)
        for rk in (2 * kt, 2 * kt + 1):
            if not (r0 <= rk < r0 + 8):
                return False
    return True


C_PAIR_IDX = {}
for _qs in range(4):
    for _kt in C_KT[_qs]:
        C_PAIR_IDX[(_kt, _qs)] = len(C_PAIR_IDX)
F_ALLONES = {k for k in C_PAIR_IDX if _f_allones(*k)}


class Res:
    __slots__ = ("name", "w", "rs", "ov", "const", "psum")

    def __init__(self, name, const=False, psum=False):
        self.name = name
        self.w = None
        self.rs = {}
        self.ov = []
        self.const = const
        self.psum = psum


class Op:
    __slots__ = ("eng", "fn", "deps", "dkey", "val", "sig", "seq")


class Prog:
    ENG = ("pe", "act", "dve", "pool", "sp")

    def __init__(self, nc, es):
        self.nc = nc
        self.es = es
        self.ops = []
        self.dcount = {}
        self.sems = {}
        self.out_keys = set()
        self.pos = {e: 0 for e in self.ENG}

    def op(self, eng, fn, reads=(), writes=(), dkey=None):
        o = Op()
        if dkey is None:
            self.pos[eng] += 1
        o.seq = self.pos[eng]
        o.eng = eng
        o.fn = fn
        o.dkey = dkey
        o.sig = False
        o.val = 0
        deps = {}
        for r in reads:
            if r.w is not None:
                deps[id(r.w)] = r.w
            if r.psum:
                for d in r.rs.values():
                    if d.eng != eng or d.dkey is not None:
                        deps[id(d)] = d
        for w in writes:
            for t in [w] + w.ov:
                if t.w is not None:
                    deps[id(t.w)] = t.w
                for d in t.rs.values():
                    deps[id(d)] = d
        dl = []
        for d in deps.values():
            if d.dkey is None and dkey is None and d.eng == eng and (eng == "pe" or not SAME_ENG_SYNC):
                continue
            if d.dkey is None and dkey is None and d.eng == eng and o.seq - d.seq >= SAME_ENG_DIST:
                continue
            dl.append(d)
        o.deps = dl
        if dkey is not None:
            self.dcount[dkey] = self.dcount.get(dkey, 0) + 16
            o.val = self.dcount[dkey]
        rkey = eng if dkey is None else "dma:" + dkey
        for r in reads:
            if not r.const:
                r.rs[rkey] = o
        for w in writes:
            w.w = o
            w.rs = {}
        self.ops.append(o)
        return o

    def sem(self, k):
        if k not in self.sems:
            name = "s_" + k[0] + "_" + k[1]
            self.sems[k] = self.es.enter_context(self.nc.semaphore(name))
        return self.sems[k]

    def finalize(self):
        nc = self.nc
        engobj = {"pe": nc.tensor, "act": nc.scalar, "dve": nc.vector, "pool": nc.gpsimd, "sp": nc.sync}
        for o in self.ops:
            for d in o.deps:
                if d.dkey is None:
                    d.sig = True
        cnt = {e: 0 for e in self.ENG}
        for o in self.ops:
            if o.dkey is None and o.sig:
                cnt[o.eng] += 1
                o.val = cnt[o.eng]
        waited = {e: {} for e in self.ENG}
        nwait = 0
        for o in self.ops:
            E = engobj[o.eng]
            need = {}
            for d in o.deps:
                k = ("e", d.eng) if d.dkey is None else ("d", d.dkey)
                if need.get(k, 0) < d.val:
                    need[k] = d.val
            for k, v in need.items():
                if waited[o.eng].get(k, 0) < v:
                    E.wait_ge(self.sem(k), v)
                    waited[o.eng][k] = v
                    nwait += 1
            ins = o.fn()
            if o.dkey is not None:
                ins.then_inc(self.sem(("d", o.dkey)), 16)
            elif o.sig:
                ins.then_inc(self.sem(("e", o.eng)), 1)
        for k in sorted(self.out_keys):
            nc.sync.wait_ge(self.sem(("d", k)), self.dcount[k])
        return dict(nops=len(self.ops), nwait=nwait, cnt=cnt, nsem=len(self.sems))

    def mm(self, out, lhsT, rhs, start, stop, reads, writes):
        nc = self.nc
        return self.op("pe", lambda: nc.tensor.matmul(out, lhsT, rhs, start=start, stop=stop), reads, writes)

    def act(self, out, in_, func, reads, writes, bias=None, scale=None):
        nc = self.nc
        kw = {}
        if bias is not None:
            kw["bias"] = bias
        if scale is not None:
            kw["scale"] = scale
        return self.op("act", lambda: nc.scalar.activation(out=out, in_=in_, func=func, **kw), reads, writes)

    def tt(self, out, in0, in1, op, reads, writes):
        nc = self.nc
        return self.op("dve", lambda: nc.vector.tensor_tensor(out=out, in0=in0, in1=in1, op=op), reads, writes)

    def ts(self, out, in0, s1, s2, op0, op1, reads, writes):
        nc = self.nc
        if op1 is None:
            return self.op("dve", lambda: nc.vector.tensor_scalar(out=out, in0=in0, scalar1=s1, scalar2=None, op0=op0),
                           reads, writes)
        return self.op("dve", lambda: nc.vector.tensor_scalar(out=out, in0=in0, scalar1=s1, scalar2=s2, op0=op0, op1=op1),
                       reads, writes)

    def stt(self, out, in0, scalar, in1, op0, op1, reads, writes):
        nc = self.nc
        return self.op("dve", lambda: nc.vector.scalar_tensor_tensor(out=out, in0=in0, scalar=scalar, in1=in1,
                                                                     op0=op0, op1=op1), reads, writes)

    def recip(self, out, in_, reads, writes):
        nc = self.nc
        return self.op("dve", lambda: nc.vector.reciprocal(out=out, in_=in_), reads, writes)

    def recipf(self, out, in_, reads, writes):
        nc = self.nc
        return self.op("dve", lambda: nc.vector.reciprocal_approx_fast(out=out, in_=in_), reads, writes)

    def copy(self, out, in_, reads, writes):
        nc = self.nc
        return self.op("dve", lambda: nc.vector.tensor_copy(out=out, in_=in_), reads, writes)

    def memset(self, ap, val, writes, eng="dve"):
        nc = self.nc
        e = nc.vector if eng == "dve" else nc.gpsimd
        return self.op(eng, lambda: e.memset(ap, val), (), writes)

    def dma(self, queue, out, in_, dkey, reads, writes, is_out=False):
        nc = self.nc
        e = {"pool": nc.gpsimd, "sp": nc.sync, "act": nc.scalar}[queue]
        if is_out:
            self.out_keys.add(dkey)
        return self.op(queue, lambda: e.dma_start(out=out, in_=in_), reads, writes, dkey=dkey)


def build_program(n_layers=DEPTH, debug=(), ddepth=DEPTH):
    nc = bass.Bass("TRN2", target_bir_lowering=False)
    es = ExitStack()
    with es:
        P = Prog(nc, es)

        def din(name, shape):
            return nc.dram_tensor(name, list(shape), F32, kind="ExternalInput").ap()

        def dout(name, shape):
            return nc.dram_tensor(name, list(shape), F32, kind="ExternalOutput").ap()

        xT_d = din("xT", [D, T])
        sm_d = din("smalls", [128, NSM])
        lqb_d = din("lqb", [128, 1024])
        cos_d = din("cosT", [128, T])
        sin_d = din("sinT", [128, T])
        perm_d = din("perm", [128, 128])
        ident_d = din("ident", [128, 128])
        ckT_d = din("ckT", [ddepth, 1152, 256])
        cv_d = din("cv", [ddepth, 256, 1152])
        rbx_d = din("rbx", [ddepth, 128, 8 * 896])
        wmod_d = din("w_mod", [ddepth, D, 6144])
        win_d = din("w_in", [ddepth, D, DIN])
        wbr_d = [din("w_ba", [ddepth, 512, D]), din("w_bb", [ddepth, 512, D]), din("w_bc", [ddepth, 512, D])]
        wo_d = din("w_o", [ddepth, D, D])
        wff1_d = din("w_ff1", [ddepth, D, DFF])
        wff2_d = din("w_ff2", [ddepth, DFF, D])
        yT_d = dout("yT", [D, T])
        nkT_d = dout("nkT", [ddepth, 1152, T])
        nv_d = dout("nv", [ddepth, T, 1152])

        def sb(name, shape, dt):
            return es.enter_context(nc.sbuf_tensor(name, list(shape), dt))

        xT = sb("xT_s", [128, 8, T], F32)
        rx = [[Res(f"x{k}_{h}") for h in range(2)] for k in range(8)]
        hT = sb("hT_s", [128, 8, T], BF16)
        rh = [[Res(f"h{k}_{h}") for h in range(2)] for k in range(8)]
        arena = sb("arena", [128, 35 * 1024], BF16)
        wring = sb("wring", [128, NSLOT, WSLOT], BF16)
        rw = []
        for s_ in range(NSLOT):
            _f, _b, _a = Res(f"w{s_}f"), Res(f"w{s_}b"), Res(f"w{s_}a")
            _f.ov = [_a, _b]
            _a.ov = [_f]
            _b.ov = [_f]
            rw.append([_f, _b, _a])
        sm = sb("sm_s", [128, NSM], F32)
        rsm = Res("sm", const=True)
        cosT = sb("cos_s", [128, T], F32)
        sinT = sb("sin_s", [128, T], F32)
        rcos = Res("cos", const=True)
        rsin = Res("sin", const=True)
        permb = sb("permb", [128, 128], BF16)
        rperm = Res("perm", const=True)
        identb = sb("identb", [128, 128], BF16)
        rident = Res("ident", const=True)
        ones = sb("ones", [128, 128], BF16)
        rones = Res("ones", const=True)
        bones = sb("bones", [128, 128], BF16)
        rbones = Res("bones", const=True)
        epsc = sb("epsc", [128, 1], F32)
        reps = Res("eps", const=True)
        cs = sb("cs", [128, 8], BF16)
        rcs = Res("cs", const=True)
        neglam = sb("neglam", [128, 4], F32)
        rnl = Res("neglam")
        lsm = sb("lsm", [128, 136], F32)
        rls = Res("lsm")
        MBt = sb("mbt", [128, 768], BF16)
        rmbt = Res("mbt")
        mraw = sb("mraw", [128, 2, 48], F32)
        rmraw = [Res("mraw0"), Res("mraw1")]
        mT = sb("mT", [128, 2, 48], F32)
        rmT = [Res("mT0"), Res("mT1")]
        modA = sb("modA", [128, 2, 16], F32)
        rmA = [Res("mA0"), Res("mA1")]
        tf = sb("tf", [128, 8, 512], F32)
        rtf = [Res(f"tf{i}") for i in range(8)]
        rstd = sb("rstd", [128, 1024], F32)
        rrstd = [Res("rstd0"), Res("rstd1")]
        sqb = sb("sqb", [128, 2, 1024], BF16)
        rsq = [Res("sq0"), Res("sq1")]
        tmpb = sb("tmpb", [128, 2, 512], BF16)
        rtb = [Res("tb0"), Res("tb1")]
        pring = sb("pring", [128, 6, 512], BF16)
        rpr = [Res(f"pr{i}") for i in range(6)]
        stage = sb("stage", [128, 2, 1024], F32)
        rst = [[Res(f"st{s}_{h}") for h in range(2)] for s in range(2)]
        ECt = sb("ec", [128, 2, 896], BF16)
        rec = [Res("ec0"), Res("ec1")]
        rcb = sb("rcb", [128, 1, 512], F32)
        rrc = [Res("rc0")]
        Onb = sb("onb", [128, 2, 512], F32)
        ron = [Res("on0"), Res("on1")]
        dfb = sb("dfb", [128, 1, 512], F32)
        rdf = [Res("df0")]

        psall = es.enter_context(nc.psum_tensor("psall", [128, 8, 512], F32))
        ps = [psall[:, i, :] for i in range(8)]
        rbk = []
        for i in range(8):
            _r = Res(f"bk{i}", psum=True)
            rbk.append([_r, _r])

        items = []

        def ares(name, start, n):
            r = Res(name)
            items.append((r, start, start + n))
            return r

        QOFF, KOFF, VOFF, OOFF = 0, 8192, 13312, 23552
        rq = [[ares(f"q{j}_{h}", QOFF + j * 2048 + h * 512, 512) for h in range(2)] for j in range(4)]
        for j in range(4):
            for h in range(2):
                items.append((rq[j][h], QOFF + j * 2048 + 1024 + h * 512, QOFF + j * 2048 + 1024 + h * 512 + 512))
        rk = [[ares(f"k{j}_{p}", KOFF + j * 1280 + (0, 512, 1024)[p], (512, 512, 256)[p]) for p in range(3)]
              for j in range(4)]
        rv = [ares(f"v{kt}", VOFF + kt * 1024, 1024) for kt in range(10)]
        ro = [[ares(f"o{c}_{q}", OOFF + c * 1024 + q * 256, 256) for q in range(4)] for c in range(12)]
        ry = [[ares(f"y{c}_{h}", c * 1024 + h * 512, 512) for h in range(2)] for c in range(8)]
        rhid = [[ares(f"hid{j}_{h}", j * 1024 + h * 512, 512) for h in range(2)] for j in range(32)]
        for (r, a, b) in items:
            for (r2, a2, b2) in items:
                if r is not r2 and a < b2 and a2 < b:
                    r.ov.append(r2)

        def QTm(j, m, c0, c1):
            return arena[:, QOFF + j * 2048 + m * 1024 + c0: QOFF + j * 2048 + m * 1024 + c1]

        def KT(j, c0, c1):
            return arena[:, KOFF + j * 1280 + c0: KOFF + j * 1280 + c1]

        def Vt(kt, c0, c1):
            return arena[:, VOFF + kt * 1024 + c0: VOFF + kt * 1024 + c1]

        def OTv(c, c0, c1):
            return arena[:, OOFF + c * 1024 + c0: OOFF + c * 1024 + c1]

        def YT(c, c0, c1):
            return arena[:, c * 1024 + c0: c * 1024 + c1]

        def HID(j, c0, c1):
            return arena[:, j * 1024 + c0: j * 1024 + c1]

        def smc(name, a, b=None):
            o, w = _SM[name]
            if b is None:
                b = a + 1
            return sm[:, o + a: o + b]

        cnt = dict(w=0, d=0, tf=0, st=0, tb=0, sq=0, pb=0, sbk=0, gs=0)

        def dbank():
            i = cnt["d"] % 4
            cnt["d"] += 1
            return i

        def nxt(name, n):
            i = cnt.get(name, 0) % n
            cnt[name] = cnt.get(name, 0) + 1
            return i

        def wload(parts):
            s = nxt("w", NSLOT)
            for pi, (off, nk, ncols, src) in enumerate(parts):
                dst = wring[:, s, off:off + nk * ncols].rearrange("p (k n) -> p k n", k=nk)
                wres = rw[s][0] if len(parts) == 1 else rw[s][2 - pi]
                P.dma("pool", dst, src, f"w{s}{'ab'[pi]}", (), [wres])
            return s

        def wview(s, off, nk, ncols):
            return wring[:, s, off:off + nk * ncols].rearrange("p (k n) -> p k n", k=nk)

        def rh_all():
            return [r for k in range(8) for r in rh[k]]

        P.dma("sp", sm[:, :], sm_d[:, :], "ld_sm", (), [rsm])
        P.dma("sp", cosT[:, :], cos_d[:, :], "ld_cos", (), [rcos])
        P.dma("sp", sinT[:, :], sin_d[:, :], "ld_sin", (), [rsin])
        for k in range(8):
            P.dma("sp", xT[:, k, :], xT_d[k * 128:(k + 1) * 128, :], f"ld_x{k}", (), rx[k])
        P.dma("pool", permb[:, :], perm_d[:, :], "ld_perm", (), [rperm])
        P.dma("pool", identb[:, :], ident_d[:, :], "ld_ident", (), [rident])
        P.dma("sp", tf[:, 0:2, :].rearrange("p a b -> p (a b)"), lqb_d[:, :], "ld_lq", (), [rtf[0], rtf[1]])
        P.memset(ones[:, :], 1.0, [rones])
        P.memset(bones[:, :], 0.0, [rbones])
        P.memset(bones[0:64, 0:64], 1.0, [rbones])
        P.memset(bones[64:128, 64:128], 1.0, [rbones])
        P.memset(epsc[:, :], EPS, [reps])
        P.act(cs[:, :], smc("cvec", 0, 8), AF.Silu, [rsm], [rcs])
        lqv = tf[:, 0:2, :].rearrange("p a b -> p (a b)")
        for l in range(DEPTH):
            lam_init = 0.8 - 0.6 * math.exp(-0.3 * l)
            b0 = l * 256
            P.tt(lsm[:, 0:64], lqv[:, b0:b0 + 64], lqv[:, b0 + 64:b0 + 128], ALU.mult, [rtf[0], rtf[1]], [rls])
            P.tt(lsm[:, 64:128], lqv[:, b0 + 128:b0 + 192], lqv[:, b0 + 192:b0 + 256], ALU.mult,
                 [rtf[0], rtf[1]], [rls])
            src = lsm[:, 0:128].rearrange("p (a b) -> p a b", a=2)
            dst = lsm[:, 128:130]
            P.op("dve", (lambda dst=dst, src=src: nc.vector.tensor_reduce(out=dst, in_=src, axis=AX.X, op=ALU.add)),
                 [rls], [rls])
            P.act(lsm[:, 130:132], lsm[:, 128:130], AF.Exp, [rls], [rls])
            P.tt(neglam[:, l:l + 1], lsm[:, 131:132], lsm[:, 130:131], ALU.subtract, [rls], [rnl])
            P.ts(neglam[:, l:l + 1], neglam[:, l:l + 1], -lam_init, None, ALU.add, None, [rnl], [rnl])

        if debug:
            P.memset(mT[:, :, :], 0.0, rmT)
            P.memset(hT[:, :, :], 0.0, rh_all())
            P.memset(arena[:, :], 0.0, [it[0] for it in items])

        def modulation_steps(l):
            par = l % 2
            bk = 7
            for b in range(12):
                src = wmod_d[l, :, b * 512:(b + 1) * 512].rearrange("(k p) n -> p k n", p=128)
                s = wload([(0, 8, 512, src)])
                wv = wview(s, 0, 8, 512)
                for jj in range(4):
                    for k in range(8):
                        P.mm(ps[bk][:, jj:jj + 1], wv[:, k, jj * 128:(jj + 1) * 128], cs[:, k:k + 1], k == 0, k == 7,
                             [rw[s][0], rcs], rbk[bk])
                P.copy(mraw[:, par, b * 4:(b + 1) * 4], ps[bk][:, 0:4], rbk[bk], [rmraw[par]])
                yield b
            ob, _ = _SM["bmod"]
            P.tt(mT[:, par, :], mraw[:, par, :], sm[:, ob + l * 48: ob + l * 48 + 48], ALU.add, [rmraw[par], rsm],
                 [rmT[par]])
            og1, _ = _SM["gn1"]
            og2, _ = _SM["gn2"]
            P.stt(modA[:, par, 0:8], mT[:, par, 8:16], 1.0, sm[:, og1 + l * 8: og1 + l * 8 + 8], ALU.add, ALU.mult,
                  [rmT[par], rsm], [rmA[par]])
            P.stt(modA[:, par, 8:16], mT[:, par, 32:40], 1.0, sm[:, og2 + l * 8: og2 + l * 8 + 8], ALU.add, ALU.mult,
                  [rmT[par], rsm], [rmA[par]])
            yield 12

        def modulation(l):
            for _ in modulation_steps(l):
                pass

        stat = {"ready": False}

        def stats_chunk(k):
            sl = nxt("sq", 2)
            P.act(sqb[:, sl, :], xT[:, k, :], AF.Square, rx[k], [rsq[sl]])
            for h in range(2):
                bk = 4 + h
                P.mm(ps[bk][:, :], ones[:, :], sqb[:, sl, h * 512:(h + 1) * 512], k == 0, k == 7,
                     [rsq[sl], rones], rbk[bk])
            if k == 7:
                stat["ready"] = True

        def rms_rstd():
            if not stat["ready"]:
                for k in range(8):
                    stats_chunk(k)
            stat["ready"] = False
            for h in range(2):
                bk = 4 + h
                P.act(rstd[:, h * 512:(h + 1) * 512], ps[bk][:, :], AF.Ln, rbk[bk] + [reps], [rrstd[h]],
                      bias=epsc[:, 0:1], scale=1.0 / D)
                P.act(rstd[:, h * 512:(h + 1) * 512], rstd[:, h * 512:(h + 1) * 512], AF.Exp, [rrstd[h]], [rrstd[h]],
                      scale=-0.5)

        def norm_mod(l, which):
            par = l % 2
            rms_rstd()
            for h in range(2):
                for k in range(8):
                    ti = nxt("tf", 8)
                    P.tt(tf[:, ti, :], xT[:, k, h * 512:(h + 1) * 512], rstd[:, h * 512:(h + 1) * 512], ALU.mult,
                         [rx[k][h], rrstd[h]], [rtf[ti]])
                    sc = modA[:, par, which * 8 + k: which * 8 + k + 1]
                    bi = mT[:, par, which * 24 + k: which * 24 + k + 1]
                    P.act(hT[:, k, h * 512:(h + 1) * 512], tf[:, ti, :], AF.Identity, [rtf[ti], rmA[par], rmT[par]],
                          [rh[k][h]], bias=bi, scale=sc)

        def zero_q_halves():
            for j in range(4):
                for m in range(2):
                    zr = 64 * (1 - m)
                    P.memset(QTm(j, m, 0, 1024)[zr:zr + 64, :], 0.0, [rq[j][0], rq[j][1]])


        def proj_mixer(l, mix, nextmod=None):
            par = l % 2
            if mix == "A":
                groups = [("q", C_QA, 4, "rope"), ("k", C_KA, 4, "rope")]
                vcol, vn, krow, vrow = C_VA, 512, 0, 0
            elif mix == "B":
                groups = [("q", C_QB, 4, "nrope"), ("k", C_KB, 1, "nrope")]
                vcol, vn, krow, vrow = C_VB, 128, 512, 512
            else:
                groups = [("q", C_QC, 4, "plain"), ("k", C_KC, 4, "plain")]
                vcol, vn, krow, vrow = C_VC, 512, 640, 640
            nkt = 4 if mix != "B" else 1
            pre = {}
            for (dst, c0, nt, kind) in groups:
                if mix == "B" and dst == "k":
                    src = win_d[l, :, C_KB:C_KB + 256].rearrange("(k p) n -> p k n", p=128)
                    s_ = wload([(0, 8, 256, src)])
                    pre[dst] = (s_, wview(s_, 0, 8, 256))
                else:
                    src = win_d[l, :, c0:c0 + 512].rearrange("(k p) n -> p k n", p=128)
                    s_ = wload([(0, 8, 512, src)])
                    pre[dst] = (s_, wview(s_, 0, 8, 512))
            if mix != "B":
                src = win_d[l, :, vcol:vcol + 512].rearrange("(k p) n -> p k n", p=128)
                s_ = wload([(0, 8, 512, src)])
                pre["v"] = (s_, wview(s_, 0, 8, 512))
            dstk = arena[:, KOFF:KOFF + nkt * 1280].rearrange("p (j n) -> p j n", j=nkt)[:, :, 1024:1280]
            srck = ckT_d[l, krow:krow + nkt * 128, :].rearrange("(j p) k -> p j k", p=128)
            import os as _os
            _nocache = _os.environ.get("DBG_NOCACHE", "0") == "1"
            if not _nocache:
                P.dma("pool", dstk, srck, "ld_ck", (), [rk[j][2] for j in range(nkt)])
                if mix == "B":
                    P.dma("pool", KT(1, 1024, 1280)[0:64, :], ckT_d[l, krow + 64:krow + 128, :], "ld_cks0", (), [rk[1][2]])
                    P.dma("pool", KT(1, 1024, 1280)[64:128, :], ckT_d[l, krow:krow + 64, :], "ld_cks1", (), [rk[1][2]])
            if mix == "A":
                dstv = arena[:, VOFF + 8 * 1024:VOFF + 10 * 1024].rearrange("p (t c) -> p t c", t=2)[:, :, 0:512]
                srcv = cv_d[l, :, 0:512].rearrange("(t p) c -> p t c", p=128)
            else:
                nh = 2 if mix == "B" else 8
                dstv = arena[:, VOFF + 8 * 1024:VOFF + 10 * 1024].rearrange("p (t h e) -> p t h e", t=2, h=8)[
                    :, :, 0:nh, 0:64]
                srcv = cv_d[l, :, vrow:vrow + nh * 64].rearrange("(t p) (h e) -> p t h e", p=128, h=nh)
                for kt in range(10):
                    vv = Vt(kt, 0, 1024).rearrange("p (h e) -> p h e", h=8)[:, 0:nh, 64:128]
                    P.memset(vv, 1.0, [rv[kt]])
            if not _nocache:
                if mix == "A":
                    P.dma("pool", dstv, srcv, "ld_cv", (), [rv[8], rv[9]])
                else:
                    for t2 in range(2):
                        dv = Vt(8 + t2, 0, 1024).rearrange("p (h e) -> p h e", h=8)[:, 0:nh, 0:64]
                        sv = cv_d[l, t2 * 128:(t2 + 1) * 128, vrow:vrow + nh * 64].rearrange("p (h e) -> p h e", h=nh)
                        P.dma("pool", dv, sv, f"ld_cv{t2}", (), [rv[8 + t2]])

            import os as _os
            _lim = _os.environ.get("DBG_PROJ", "qkv")
            if "k" not in _lim:
                groups = groups[:1]
            its = []
            for (dst, c0, nt, kind) in groups:
                s, wv = pre[dst]
                if mix == "B" and dst == "k":
                    vslot = (s, wv, 128)
                for j in range(nt):
                    for h in range(2):
                        its.append(dict(dst=dst, j=j, h=h, kind=kind, s=s, wv=wv, col=j * 128))
                if mix == "B" and dst == "k":
                    for h in range(2):
                        its.append(dict(dst="ks", j=1, h=h, kind=kind, s=s, wv=wv, col=0))

            def stage1(it):
                bk = dbank()
                it["bk"] = bk
                h = it["h"]
                if it["dst"] == "ks":
                    for (po, co) in ((0, 64), (64, 0)):
                        for k in range(8):
                            P.mm(ps[bk][po:po + 64, :], it["wv"][:, k, co:co + 64], hT[:, k, h * 512:(h + 1) * 512],
                                 k == 0, k == 7, [rw[it["s"]][0], rh[k][h]], rbk[bk])
                    return
                for k in range(8):
                    P.mm(ps[bk][:, :], it["wv"][:, k, it["col"]:it["col"] + 128], hT[:, k, h * 512:(h + 1) * 512],
                         k == 0, k == 7, [rw[it["s"]][0], rh[k][h]], rbk[bk])

            _s2 = int(_os.environ.get("DBG_S2", "9"))

            def stage2(it):
                bk, h, j, kind, dst = it["bk"], it["h"], it["j"], it["kind"], it["dst"]
                if _s2 == 0:
                    return
                isk = dst == "k"
                isks = dst == "ks"
                c0, c1 = h * 512, (h + 1) * 512
                main = ps[bk][:, :]
                rmain = rbk[bk]
                if isk:
                    if h == 0:
                        it["sl"] = nxt("st", 2)
                        stsl[0] = it["sl"]
                    sl = stsl[0]
                    outf = stage[:, sl, c0:c1]
                    routf = [rst[sl][h]]
                    outb = KT(j, c0, c1)
                    routb = [rk[j][h]]
                elif isks:
                    outb = KT(1, c0, c1)
                    routb = [rk[1][h]]
                else:
                    outb = None
                    routb = [rq[j][h]]
                isq = outb is None

                def qparts():
                    return [(pr, QTm(j, pr // 64, c0, c1)[pr:pr + 64, :]) for pr in (0, 64)]
                if kind == "plain":
                    if isk:
                        P.act(outf, main, AF.Identity, rmain, routf)
                    if isq:
                        for pr, ob_ in qparts():
                            P.ts(ob_, ps[bk][pr:pr + 64, :], 0.125, None, ALU.mult, None, rmain, routb)
                    else:
                        P.copy(outb, main, rmain, routb)
                else:
                    tb = nxt("tb", 2)
                    pb = 4 + nxt("pb", 2)
                    if kind == "rope":
                        P.act(tmpb[:, tb, :], main, AF.Identity, rmain, [rtb[tb]])
                        usrc, rusrc = main, rmain
                    else:
                        sl2 = nxt("sq", 2)
                        sbk = 6
                        P.act(sqb[:, sl2, 0:512], main, AF.Square, rmain, [rsq[sl2]])
                        P.mm(ps[sbk][:, :], bones[:, :], sqb[:, sl2, 0:512], True, True, [rsq[sl2], rbones], rbk[sbk])
                        gcol = smc("gk" if (isk or isks) else "gq", l)
                        P.act(tmpb[:, tb, :], main, AF.Identity, rmain + [rsm], [rtb[tb]], scale=gcol)
                        usrc, rusrc = None, None
                        si = nxt("tf", 8)
                        P.act(tf[:, si, :], ps[sbk][:, :], AF.Ln, rbk[sbk] + [reps], [rtf[si]], bias=epsc[:, 0:1],
                              scale=1.0 / 64)
                        P.act(tf[:, si, :], tf[:, si, :], AF.Exp, [rtf[si]], [rtf[si]], scale=-0.5)
                    if _s2 == 1:
                        return
                    P.mm(ps[pb][:, :], permb[:, :], tmpb[:, tb, :], True, True, [rtb[tb], rperm], rbk[pb])
                    if _s2 == 2:
                        return
                    t1 = nxt("tf", 8)
                    t2 = nxt("tf", 8)
                    if usrc is None:
                        P.stt(tf[:, t1, :], main, gcol, cosT[:, c0:c1], ALU.mult, ALU.mult, rmain + [rsm, rcos],
                              [rtf[t1]])
                    else:
                        P.tt(tf[:, t1, :], usrc, cosT[:, c0:c1], ALU.mult, rusrc + [rcos], [rtf[t1]])
                    P.tt(tf[:, t2, :], ps[pb][:, :], sinT[:, c0:c1], ALU.mult, rbk[pb] + [rsin], [rtf[t2]])
                    if _s2 == 3:
                        return
                    if kind == "rope":
                        if isk:
                            P.tt(outf, tf[:, t1, :], tf[:, t2, :], ALU.add, [rtf[t1], rtf[t2]], routf)
                            P.act(outb, outf, AF.Identity, routf, routb)
                        elif isq:
                            for pr, ob_ in qparts():
                                P.tt(ob_, tf[pr:pr + 64, t1, :], tf[pr:pr + 64, t2, :], ALU.add, [rtf[t1], rtf[t2]], routb)
                        else:
                            P.tt(outb, tf[:, t1, :], tf[:, t2, :], ALU.add, [rtf[t1], rtf[t2]], routb)
                    else:
                        P.tt(tf[:, t1, :], tf[:, t1, :], tf[:, t2, :], ALU.add, [rtf[t1], rtf[t2]], [rtf[t1]])
                        if isk:
                            P.tt(outf, tf[:, t1, :], tf[:, si, :], ALU.mult, [rtf[t1], rtf[si]], routf)
                            P.act(outb, outf, AF.Identity, routf, routb)
                        elif isq:
                            for pr, ob_ in qparts():
                                P.tt(ob_, tf[pr:pr + 64, t1, :], tf[pr:pr + 64, si, :], ALU.mult, [rtf[t1], rtf[si]], routb)
                        else:
                            P.tt(outb, tf[:, t1, :], tf[:, si, :], ALU.mult, [rtf[t1], rtf[si]], routb)
                if isk and h == 1:
                    r0 = krow + j * 128
                    P.dma("sp", nkT_d[l, r0:r0 + 128, :], stage[:, sl, :], f"st{sl}", rst[sl], (), is_out=True)

            stsl = [0]
            for i, it in enumerate(its):
                stage1(it)
                if i >= 1:
                    stage2(its[i - 1])
            if nextmod is not None:
                next(nextmod, None)
            if "v" not in _lim:
                stage2(its[-1])
                return
            if mix == "B":
                s, wv, vc0 = vslot
            else:
                s, wv = pre["v"]
                vc0 = 0
            prev = None
            for t in range(8):
                bk = dbank()
                for k in range(8):
                    P.mm(ps[bk][:, 0:vn], hT[:, k, t * 128:(t + 1) * 128], wv[:, k, vc0:vc0 + vn], k == 0, k == 7,
                         [rw[s][0], rh[k][t // 4]], rbk[bk])
                if t == 0:
                    stage2(its[-1])
                sl = nxt("st", 2)
                P.act(stage[:, sl, 0:vn], ps[bk][:, 0:vn], AF.Identity, rbk[bk], rst[sl])
                if mix == "A":
                    P.copy(Vt(t, 0, 512), ps[bk][:, 0:512], rbk[bk], [rv[t]])
                else:
                    nh = 2 if mix == "B" else 8
                    vv = Vt(t, 0, 1024).rearrange("p (h e) -> p h e", h=8)[:, 0:nh, 0:64]
                    P.copy(vv, ps[bk][:, 0:vn].rearrange("p (h e) -> p h e", h=nh), rbk[bk], [rv[t]])
                P.dma("sp", nv_d[l, t * 128:(t + 1) * 128, vrow:vrow + vn], stage[:, sl, 0:vn], f"st{sl}", rst[sl], (),
                      is_out=True)
            if nextmod is not None:
                next(nextmod, None)

        gsl = sb("gsl", [128, 4], F32)
        rgsl = Res("gsl")
        for l in range(DEPTH):
            lam_init = 0.8 - 0.6 * math.exp(-0.3 * l)
            P.ts(gsl[:, l:l + 1], smc("gsub", l), 1.0 - lam_init, None, ALU.mult, None, [rsm], [rgsl])

        def attention(l, mix):
            mbo, _ = _SM["mb"]
            fmo, _ = _SM["fm"]
            groups = []
            if mix == "A":
                for h in range(4):
                    for qp in range(2):
                        for m in (h, h + 4):
                            groups.append((dict(h=h, m=m, jq=m // 2, bq=(m % 2) * 64, jk=m // 2, vcol=h * 128), qp))
            elif mix == "B":
                for h in range(8):
                    for qp in range(2):
                        jkb = 0 if (h % 2) == (h // 4) else 1
                        groups.append((dict(h=h, jq=h // 2, bq=(h % 2) * 64, jk=jkb, vcol=(h // 4) * 128,
                                            oc=4 + h // 2, ob=(h % 2) * 64), qp))
            else:
                for h in range(8):
                    for qp in range(2):
                        groups.append((dict(h=h, jq=h // 2, bq=(h % 2) * 64, jk=h // 2, vcol=h * 128,
                                            oc=8 + h // 2, ob=(h % 2) * 64), qp))
            tiles = []
            for g, (u, qp) in enumerate(groups):
                if mix == "C":
                    kts = sorted(set(C_KT[2 * qp]) | set(C_KT[2 * qp + 1])) + [8, 9]
                else:
                    kts = list(range(10))
                for i, kt in enumerate(kts):
                    tiles.append(dict(u=u, qp=qp, kt=kt, first=(i == 0), last=(i == len(kts) - 1), g=g))

            def build_ec(h):
                e = h % 2
                P.dma("pool", ECt[:, e, :], rbx_d[l, :, h * 896:(h + 1) * 896], f"ld_rb{e}", (), [rec[e]])

            pend = []
            LOOKP = 1 if mix == "A" else 2
            pemask = mix != "A"
            if pemask:
                P.memset(MBt[:, :], 0.0, [rmbt])
                mcross = sm[:, mbo + 1: mbo + 2]
                for c0_ in (0, 512):
                    P.ts(MBt[:, c0_:c0_ + 256], MBt[:, c0_:c0_ + 256], mcross, (1.0 if mix == "C" else 8.0), ALU.add,
                         ALU.mult, [rmbt, rsm], [rmbt])
            esc = 1.0 if mix == "C" else 0.125
            spairs = [2, 4] if mix == "A" else [0, 2, 4]
            mpairs = [(tiles[2 * a_], tiles[2 * a_ + 1]) for a_ in range(len(tiles) // 2)]

            def exp_tile(t, bkS, pi):
                u, qp, kt = t["u"], t["qp"], t["kt"]
                ks = kt // 2 if kt < 8 else 4
                same_cls = (ks == 4) or (ks not in (2 * qp, 2 * qp + 1))
                cvalid = [(mix != "C") or kt >= 8 or ((kt, 2 * qp + hq) in C_PAIR_IDX) for hq in range(2)]
                if (same_cls or t.get("pm")) and all(cvalid):
                    bq = ks if t.get("pm") else 2 * qp
                    bcol = sm[:, mbo + ks * 4 + bq: mbo + ks * 4 + bq + 1]
                    P.act(pring[:, pi, :], ps[bkS][:, :], AF.Exp, rbk[bkS] + [rsm], [rpr[pi]], bias=bcol, scale=esc)
                else:
                    for hq in range(2):
                        qs = 2 * qp + hq
                        c0, c1 = hq * 256, (hq + 1) * 256
                        if not cvalid[hq]:
                            P.memset(pring[:, pi, c0:c1], 0.0, [rpr[pi]])
                            continue
                        bcol = sm[:, mbo + ks * 4 + qs: mbo + ks * 4 + qs + 1]
                        P.act(pring[:, pi, c0:c1], ps[bkS][:, c0:c1], AF.Exp, rbk[bkS] + [rsm], [rpr[pi]], bias=bcol,
                              scale=esc)

            def fmul_tile(t, pi):
                u, qp, kt = t["u"], t["qp"], t["kt"]
                if not (mix == "C" and kt < 8):
                    return
                for hq in range(2):
                    qs = 2 * qp + hq
                    c0, c1 = hq * 256, (hq + 1) * 256
                    if (kt, qs) not in C_PAIR_IDX or (kt, qs) in F_ALLONES:
                        continue
                    idx = C_PAIR_IDX[(kt, qs)]
                    fv = sm[:, fmo + idx * 4: fmo + idx * 4 + 4].unsqueeze(2).broadcast_to([128, 4, 64])
                    prv = pring[:, pi, c0:c1].rearrange("p (r c) -> p r c", r=4)
                    P.tt(prv, prv, fv, ALU.mult, [rpr[pi], rsm], [rpr[pi]])

            def emitS_pair(ip):
                b0 = spairs[ip % len(spairs)]
                s0 = 2 * (ip % 3)
                tp = mpairs[ip]
                for x, t in enumerate(tp):
                    u, qp, kt = t["u"], t["qp"], t["kt"]
                    bkS = b0 + x
                    t["pi"] = s0 + x
                    kpart = kt // 4 if kt < 8 else 2
                    lhsT = KT(u["jk"], kt * 128, (kt + 1) * 128)
                    rhs = QTm(u["jq"], u["bq"] // 64, qp * 512, (qp + 1) * 512)
                    cbias = (mix == "C" and kt < 8)
                    ks_ = kt // 2 if kt < 8 else 4
                    val_ = [(mix != "C") or kt >= 8 or ((kt, 2 * qp + hq) in C_PAIR_IDX) for hq in range(2)]
                    t["pm"] = pemask and ks_ in (2 * qp, 2 * qp + 1) and all(val_)
                    extra = []
                    if False and x == 0 and mix != "A" and ip >= LOOKP:
                        jp = ip - LOOKP
                        extra = [rpr[2 * (jp % 3)], rpr[2 * (jp % 3) + 1]]
                    P.mm(ps[bkS][:, :], lhsT, rhs, True, not (cbias or t["pm"]),
                         [rk[u["jk"]][kpart], rq[u["jq"]][qp]] + extra, rbk[bkS])
                    if cbias:
                        e = u["h"] % 2
                        val = [((kt, 2 * qp + hq) in C_PAIR_IDX) for hq in range(2)]
                        u0a = 6 - 2 * kt + 4 * (2 * qp)
                        if val[0] and val[1]:
                            P.mm(ps[bkS][:, :], identb[:, :], ECt[:, e, u0a * 64:u0a * 64 + 512], False, not t["pm"],
                                 [rec[e], rident], rbk[bkS])
                        else:
                            hq = 0 if val[0] else 1
                            u0h = u0a + 4 * hq
                            P.mm(ps[bkS][:, hq * 256:(hq + 1) * 256], identb[:, :],
                                 ECt[:, e, u0h * 64:u0h * 64 + 256], False, True, [rec[e], rident], rbk[bkS])
                for x, t in enumerate(tp):
                    if t["pm"]:
                        ks_ = t["kt"] // 2
                        off = 0 if ks_ == 2 * t["qp"] + 1 else 256
                        P.mm(ps[b0 + x][:, :], identb[:, :], MBt[:, off:off + 512], False, True, [rmbt, rident],
                             rbk[b0 + x])
                t0 = tp[0]
                qp, kt0 = t0["qp"], t0["kt"]
                ks = kt0 // 2 if kt0 < 8 else 4
                same_cls = (ks == 4) or (ks not in (2 * qp, 2 * qp + 1))
                allv = all((mix != "C") or t["kt"] >= 8 or ((t["kt"], 2 * qp + hq) in C_PAIR_IDX)
                           for t in tp for hq in range(2))
                pmall = all(t["pm"] for t in tp)
                if (same_cls or pmall) and allv:
                    bq = ks if pmall else 2 * qp
                    bcol = sm[:, mbo + ks * 4 + bq: mbo + ks * 4 + bq + 1]
                    P.act(pring[:, s0:s0 + 2, :], psall[:, b0:b0 + 2, :], AF.Exp, rbk[b0] + rbk[b0 + 1] + [rsm],
                          [rpr[s0], rpr[s0 + 1]], bias=bcol, scale=esc)
                else:
                    for x, t in enumerate(tp):
                        exp_tile(t, b0 + x, s0 + x)
                for x, t in enumerate(tp):
                    fmul_tile(t, s0 + x)

            def post_a2(h, qp, sl2, bk):
                P.mm(ps[bk][:, :], ones[:, :], sqb[:, sl2, 0:512], True, True, [rsq[sl2], rones], rbk[bk])
                si = nxt("tf", 8)
                P.act(tf[:, si, :], ps[bk][:, :], AF.Ln, rbk[bk] + [reps], [rtf[si]], bias=epsc[:, 0:1],
                      scale=1.0 / 128)
                P.act(tf[:, si, :], tf[:, si, :], AF.Exp, [rtf[si]], [rtf[si]], scale=-0.5)
                P.stt(OTv(h, qp * 512, (qp + 1) * 512), dfb[:, 0, :], gsl[:, l:l + 1], tf[:, si, :], ALU.mult,
                      ALU.mult, [rdf[0], rtf[si], rgsl], [ro[h][2 * qp], ro[h][2 * qp + 1]])

            def emitPV(t, step):
                u, qp, kt, g, pi = t["u"], t["qp"], t["kt"], t["g"], t["pi"]
                sl = g % 2
                bO = 6 + sl
                bD = sl
                P.mm(ps[bO][:, :], Vt(kt, u["vcol"], u["vcol"] + 128), pring[:, pi, :], t["first"], t["last"],
                     [rv[kt], rpr[pi]], rbk[bO])
                if mix == "A":
                    P.mm(ps[bD][:, :], ones[:, :], pring[:, pi, :], t["first"], t["last"], [rpr[pi], rones], rbk[bD])
                if not t["last"]:
                    return
                if mix == "A":
                    P.recip(rcb[:, 0, :], ps[bD][:, :], rbk[bD], [rrc[0]])
                    mi = 0 if u["m"] < 4 else 1
                    P.tt(Onb[:, mi, :], ps[bO][:, :], rcb[:, 0, :], ALU.mult, rbk[bO] + [rrc[0]], [ron[mi]])
                    if mi == 1:
                        h = u["h"]
                        st_ = {}

                        def post_a1(st_=st_):
                            P.stt(dfb[:, 0, :], Onb[:, 1, :], neglam[:, l:l + 1], Onb[:, 0, :], ALU.mult, ALU.add,
                                  [ron[0], ron[1], rnl], [rdf[0]])
                            st_["sl2"] = nxt("sq", 2)
                            P.act(sqb[:, st_["sl2"], 0:512], dfb[:, 0, :], AF.Square, [rdf[0]], [rsq[st_["sl2"]]])
                        pend.append((step + 3, post_a1))
                        pend.append((step + 5, (lambda h=h, qp=qp, st_=st_, bD=bD: post_a2(h, qp, st_["sl2"], bD))))
                else:
                    ob = u["ob"]
                    P.recip(rcb[0:64, 0, :], ps[bO][64:128, :], rbk[bO], [rrc[0]])
                    P.tt(OTv(u["oc"], qp * 512, (qp + 1) * 512)[ob:ob + 64, :], ps[bO][0:64, :], rcb[0:64, 0, :],
                         ALU.mult, rbk[bO] + [rrc[0]], [ro[u["oc"]][2 * qp], ro[u["oc"]][2 * qp + 1]])

            LOOKP = 1 if mix == "A" else 2
            npair = len(mpairs)
            if mix == "C":
                build_ec(0)
            for ip in range(npair + LOOKP):
                if ip < npair:
                    t = mpairs[ip][0]
                    if mix == "C" and t["first"] and t["qp"] == 0 and t["u"]["h"] + 1 < 8:
                        build_ec(t["u"]["h"] + 1)
                    emitS_pair(ip)
                j = ip - LOOKP
                if j >= 0:
                    emitPV(mpairs[j][0], ip)
                    emitPV(mpairs[j][1], ip)
                pend.sort(key=lambda x: x[0])
                while pend and pend[0][0] <= ip:
                    pend.pop(0)[1]()
            pend.sort(key=lambda x: x[0])
            while pend:
                pend.pop(0)[1]()

        def merge(l):
            obg, _ = _SM["bgate"]
            for cg in range(4):
                slots = []
                for br in range(3):
                    g0 = C_G + br * 1024 + cg * 256
                    gsrc = win_d[l, :, g0:g0 + 256].rearrange("(k p) n -> p k n", p=128)
                    bsrc = wbr_d[br][l, :, cg * 256:(cg + 1) * 256].rearrange("(k p) n -> p k n", p=128)
                    slots.append(wload([(0, 8, 256, gsrc), (2048, 4, 256, bsrc)]))
                for br in range(3):
                    s = slots[br]
                    gv = wview(s, 0, 8, 256)
                    bv = wview(s, 2048, 4, 256)
                    for cc in range(2):
                        c = cg * 2 + cc
                        for h in range(2):
                            acc = 4 + cc * 2 + h
                            bg = dbank()
                            for k in range(8):
                                P.mm(ps[bg][:, :], gv[:, k, cc * 128:(cc + 1) * 128], hT[:, k, h * 512:(h + 1) * 512],
                                     k == 0, k == 7, [rw[s][2], rh[k][h]], rbk[bg])
                            bp = dbank()
                            for k in range(4):
                                oc = br * 4 + k
                                P.mm(ps[bp][:, :], bv[:, k, cc * 128:(cc + 1) * 128], OTv(oc, h * 512, (h + 1) * 512),
                                     k == 0, k == 3, [rw[s][1], ro[oc][2 * h], ro[oc][2 * h + 1]], rbk[bp])
                            gi = nxt("mtf", 4)
                            bcol = sm[:, obg + l * 24 + br * 8 + c: obg + l * 24 + br * 8 + c + 1]
                            P.act(tf[:, gi, :], ps[bg][:, :], AF.Sigmoid, rbk[bg] + [rsm], [rtf[gi]], bias=bcol)
                            if br == 0:
                                P.tt(tf[:, acc, :], ps[bp][:, :], tf[:, gi, :], ALU.mult, rbk[bp] + [rtf[gi]],
                                     [rtf[acc]])
                            else:
                                b = nxt("mtf", 4)
                                P.tt(tf[:, b, :], ps[bp][:, :], tf[:, gi, :], ALU.mult, rbk[bp] + [rtf[gi]], [rtf[b]])
                                if br == 1:
                                    P.tt(tf[:, acc, :], tf[:, acc, :], tf[:, b, :], ALU.add, [rtf[acc], rtf[b]],
                                         [rtf[acc]])
                                else:
                                    P.tt(YT(c, h * 512, (h + 1) * 512), tf[:, acc, :], tf[:, b, :], ALU.add,
                                         [rtf[acc], rtf[b]], [ry[c][h]])

        def wo_phase(l):
            par = l % 2
            for cb in range(2):
                src = wo_d[l, :, cb * 512:(cb + 1) * 512].rearrange("(k p) n -> p k n", p=128)
                s = wload([(0, 8, 512, src)])
                wv = wview(s, 0, 8, 512)
                for cc in range(4):
                    c = cb * 4 + cc
                    for h in range(2):
                        bk = dbank()
                        for k in range(8):
                            P.mm(ps[bk][:, :], wv[:, k, cc * 128:(cc + 1) * 128], YT(k, h * 512, (h + 1) * 512), k == 0,
                                 k == 7, [rw[s][0], ry[k][h]], rbk[bk])
                        xs = xT[:, c, h * 512:(h + 1) * 512]
                        P.stt(xs, ps[bk][:, :], mT[:, par, 16 + c:17 + c], xs, ALU.mult, ALU.add,
                              rbk[bk] + [rmT[par], rx[c][h]], [rx[c][h]])
                    if not debug:
                        stats_chunk(c)

        def ffn(l, nextmod=None):
            par = l % 2
            norm_mod(l, 1)

            def adv(n=1):
                if nextmod is not None:
                    for _ in range(n):
                        next(nextmod, None)
            for b in range(8):
                src = wff1_d[l, :, b * 512:(b + 1) * 512].rearrange("(k p) n -> p k n", p=128)
                s = wload([(0, 8, 512, src)])
                wv = wview(s, 0, 8, 512)
                for jj in range(4):
                    j = b * 4 + jj
                    for h in range(2):
                        bk = dbank()
                        for k in range(8):
                            P.mm(ps[bk][:, :], wv[:, k, jj * 128:(jj + 1) * 128], hT[:, k, h * 512:(h + 1) * 512], k == 0,
                                 k == 7, [rw[s][0], rh[k][h]], rbk[bk])
                        ti = nxt("tf", 8)
                        P.act(tf[:, ti, :], ps[bk][:, :], AF.Relu, rbk[bk], [rtf[ti]])
                        P.tt(HID(j, h * 512, (h + 1) * 512), tf[:, ti, :], tf[:, ti, :], ALU.mult, [rtf[ti]],
                             [rhid[j][h]])
                adv(1)
            for c in range(8):
                src = wff2_d[l, :, c * 128:(c + 1) * 128].rearrange("(j p) n -> p j n", p=128)
                s = wload([(0, 32, 128, src)])
                wv = wview(s, 0, 32, 128)
                for h in range(2):
                    bk = dbank()
                    for j in range(32):
                        P.mm(ps[bk][:, :], wv[:, j, :], HID(j, h * 512, (h + 1) * 512), j == 0, j == 31,
                             [rw[s][0], rhid[j][h]], rbk[bk])
                    xs = xT[:, c, h * 512:(h + 1) * 512]
                    P.stt(xs, ps[bk][:, :], mT[:, par, 40 + c:41 + c], xs, ALU.mult, ALU.add,
                          rbk[bk] + [rmT[par], rx[c][h]], [rx[c][h]])
                if not debug:
                    stats_chunk(c)
                adv(1)
            adv(20)

        def final_norm():
            rms_rstd()
            ogf, _ = _SM["gfin"]
            for k in range(8):
                sl = nxt("st", 2)
                for h in range(2):
                    P.stt(stage[:, sl, h * 512:(h + 1) * 512], xT[:, k, h * 512:(h + 1) * 512],
                          sm[:, ogf + k:ogf + k + 1], rstd[:, h * 512:(h + 1) * 512], ALU.mult, ALU.mult,
                          [rx[k][h], rrstd[h], rsm], [rst[sl][h]])
                P.dma("sp", yT_d[k * 128:(k + 1) * 128, :], stage[:, sl, :], f"st{sl}", rst[sl], (), is_out=True)


        stop = debug[0] if debug else None
        done = False
        for l in range(n_layers):
            nm = modulation_steps(l + 1) if (l + 1 < n_layers and not debug) else None
            phases = [("mod", lambda: ((modulation(l) if (l == 0 or debug) else None), zero_q_halves())),
                      ("norm1", lambda: norm_mod(l, 0)),
                      ("projA", lambda: proj_mixer(l, "A", nm)), ("attnA", lambda: attention(l, "A")),
                      ("projB", lambda: proj_mixer(l, "B", nm)), ("attnB", lambda: attention(l, "B")),
                      ("projC", lambda: proj_mixer(l, "C", nm)), ("attnC", lambda: attention(l, "C")),
                      ("merge", lambda: merge(l)), ("wo", lambda: wo_phase(l)), ("ffn", lambda: ffn(l, nm))]
            for name, fn in phases:
                fn()
                if stop == (l, name):
                    done = True
                    break
            if done:
                break
        if not done:
            final_norm()
        if debug:
            dxT = dout("dbg_xT", [128, 8 * T])
            dmT = dout("dbg_mT", [128, 96])
            dhT = nc.dram_tensor("dbg_hT", [128, 8 * T], BF16, kind="ExternalOutput").ap()
            dar = nc.dram_tensor("dbg_arena", [128, 35 * 1024], BF16, kind="ExternalOutput").ap()
            allx = [r for k in range(8) for r in rx[k]]
            P.dma("sp", dxT[:, :], xT[:, :, :].rearrange("p a b -> p (a b)"), "dbg0", allx, (), is_out=True)
            P.dma("sp", dmT[:, :], mT[:, :, :].rearrange("p a b -> p (a b)"), "dbg1", rmT, (), is_out=True)
            P.dma("sp", dhT[:, :], hT[:, :, :].rearrange("p a b -> p (a b)"), "dbg2", rh_all(), (), is_out=True)
            P.dma("sp", dar[:, :], arena[:, :], "dbg3", [it[0] for it in items], (), is_out=True)
        stats = P.finalize()
    return nc, stats


GRID_W = 64
ROPE_BASE = 10000.0


def _rope_tables():
    nf = 16
    t = np.arange(T)
    row = (t // GRID_W).astype(np.float32)
    col = (t % GRID_W).astype(np.float32)
    inv = (np.float32(ROPE_BASE) ** (-np.arange(nf, dtype=np.float32) / np.float32(nf))).astype(np.float32)
    cosT = np.zeros((128, T), np.float32)
    sinT = np.zeros((128, T), np.float32)
    for p in range(128):
        d = p % 64
        pos = row if d < 32 else col
        ang = (pos * inv[d % 16]).astype(np.float32)
        cosT[p] = np.cos(ang)
        s = np.sin(ang)
        sinT[p] = -s if (d % 32) < 16 else s
    return cosT, sinT


def _perm():
    pm = np.zeros((128, 128), np.float32)
    for d in range(128):
        k = d + 16 if (d % 32) < 16 else d - 16
        pm[k, d] = 1.0
    return pm


def _fm_table(sample):
    fm = np.zeros((128, 80), np.float32)
    for (kt, qs), idx in C_PAIR_IDX.items():
        for p in range(128):
            rk = 2 * kt + p // 64
            for rqi in range(4):
                rq = 4 * qs + rqi
                if sample:
                    r0 = min(max(rq - 4, 0), 8)
                    ok = r0 <= rk < r0 + 8
                else:
                    ok = (kt // 2) == qs
                fm[p, idx * 4 + rqi] = 1.0 if ok else 0.0
    return fm


def _rbx_table(rel_bias):
    p = np.arange(128)
    ck = p % 64
    half = p // 64
    u = np.arange(14)
    cq = np.arange(64)
    dr = 13 - u[None, :] + half[:, None]
    dr_ok = (dr >= 0) & (dr <= 14)
    drc = np.clip(dr, 0, 14)
    dc = np.clip(ck[:, None] - cq[None, :] + 15, 0, 30)
    c0 = np.clip(cq - 8, 0, 48)
    win = (ck[:, None] >= c0[None, :]) & (ck[:, None] < c0[None, :] + 16)
    g = rel_bias[:, :, drc[:, :, None], dc[:, None, :]]
    ok = dr_ok[:, :, None] & win[:, None, :]
    g = np.where(ok[None, None], g, np.float32(NEGM)).astype(np.float32)
    g = np.transpose(g, (0, 2, 1, 3, 4)).reshape(DEPTH, 128, 8 * 896)
    return np.ascontiguousarray(g)


def _prep(inputs):
    f = lambda a: np.ascontiguousarray(np.asarray(a, dtype=np.float32))
    I = {k: f(v) for k, v in inputs.items()}
    cos_s, sin_s = _rope_tables()
    cos_p = np.ones((128, T), np.float32)
    sin_p = np.zeros((128, T), np.float32)
    perm = _perm()
    lqb = np.ascontiguousarray(np.broadcast_to(I["lambda_qk"].reshape(1, 1024), (128, 1024)))
    rbx_s = _rbx_table(I["rel_bias"])
    rbx_p = np.zeros_like(rbx_s)
    fm_s = _fm_table(True)
    fm_p = _fm_table(False)
    mb_s = np.zeros((128, 20), np.float32)
    mb_p = np.full((128, 20), NEGM, np.float32)
    for s_ in range(4):
        mb_p[:, s_ * 4 + s_] = 0.0

    def put(sm, name, arr):
        o, w = _SM[name]
        assert arr.shape == (128, w), (name, arr.shape, w)
        sm[:, o:o + w] = arr

    base = np.zeros((128, NSM), np.float32)
    put(base, "bmod", I["b_mod"].reshape(4, 48, 128).transpose(2, 0, 1).reshape(128, 192))
    put(base, "gn1", I["g_norm1"].reshape(4, 8, 128).transpose(2, 0, 1).reshape(128, 32))
    put(base, "gn2", I["g_norm2"].reshape(4, 8, 128).transpose(2, 0, 1).reshape(128, 32))
    put(base, "gfin", I["g_final"].reshape(8, 128).T)
    put(base, "bgate", I["b_gate"].reshape(4, 24, 128).transpose(2, 0, 1).reshape(128, 96))
    put(base, "gsub", I["g_subln"].T)
    put(base, "gq", np.tile(I["g_qnorm"].T, (2, 1)))
    put(base, "gk", np.tile(I["g_knorm"].T, (2, 1)))
    shared = dict(lqb=lqb, perm=perm, ident=np.eye(128, dtype=np.float32), w_mod=I["w_mod"], w_in=I["w_in"], w_ba=I["w_branch_a"], w_bb=I["w_branch_b"],
                  w_bc=I["w_branch_c"], w_o=I["w_out"], w_ff1=I["w_ff1"], w_ff2=I["w_ff2"])
    zck = np.zeros((DEPTH, 1152, 256), np.float32)
    zcv = np.zeros((DEPTH, 256, 1152), np.float32)
    maps = []
    for c in range(NCORE):
        sm = base.copy()
        m = dict(shared)
        if c < 4:
            b = c
            m["xT"] = np.ascontiguousarray(I["x_sample"][b].T)
            put(sm, "cvec", I["c"][b].reshape(8, 128).T)
            put(sm, "mb", mb_s)
            put(sm, "fm", fm_s)
            ck = np.concatenate([I["cache_a_k"][b].reshape(DEPTH, 256, 512), I["cache_b_k"][b].reshape(DEPTH, 256, 128),
                                 I["cache_c_k"][b].reshape(DEPTH, 256, 512)], axis=2)
            m["ckT"] = np.ascontiguousarray(ck.transpose(0, 2, 1))
            m["cv"] = np.ascontiguousarray(np.concatenate(
                [I["cache_a_v"][b].reshape(DEPTH, 256, 512), I["cache_b_v"][b].reshape(DEPTH, 256, 128),
                 I["cache_c_v"][b].reshape(DEPTH, 256, 512)], axis=2))
            m["cosT"], m["sinT"], m["rbx"] = cos_s, sin_s, rbx_s
        else:
            j = c - 4
            m["xT"] = np.ascontiguousarray(I["x_prompt"][4 * j:4 * j + 4].reshape(T, D).T)
            put(sm, "cvec", I["c_ctx"].reshape(8, 128).T)
            put(sm, "mb", mb_p)
            put(sm, "fm", fm_p)
            m["ckT"], m["cv"] = zck, zcv
            m["cosT"], m["sinT"], m["rbx"] = cos_p, sin_p, rbx_p
        m["smalls"] = sm
        maps.append(m)
    return maps


_PROG = {}


def _get_prog():
    if "nc" not in _PROG:
        _PROG["nc"], _PROG["stats"] = build_program()
    return _PROG["nc"]


def kernel(**inputs):
    maps = _prep(inputs)
    nc = _get_prog()
    res = run_bass_kernel_spmd(nc, maps, core_ids=list(range(NCORE)))
    R = res.results
    y_sample = np.stack([np.asarray(R[b]["yT"], np.float32).T for b in range(4)], axis=0)
    y_prompt = np.concatenate([np.asarray(R[4 + j]["yT"], np.float32).T.reshape(4, 256, D) for j in range(4)], axis=0)
    nk = [np.asarray(R[4 + j]["nkT"], np.float32) for j in range(4)]
    nv = [np.asarray(R[4 + j]["nv"], np.float32) for j in range(4)]

    def kout(r0, r1, hh, dd):
        out = np.empty((16, DEPTH, 256, hh, dd), np.float32)
        for j in range(4):
            a = nk[j][:, r0:r1, :].transpose(0, 2, 1)
            out[4 * j:4 * j + 4] = a.reshape(DEPTH, 4, 256, hh, dd).transpose(1, 0, 2, 3, 4)
        return out

    def vout(c0, c1, hh, dd):
        out = np.empty((16, DEPTH, 256, hh, dd), np.float32)
        for j in range(4):
            a = nv[j][:, :, c0:c1]
            out[4 * j:4 * j + 4] = a.reshape(DEPTH, 4, 256, hh, dd).transpose(1, 0, 2, 3, 4)
        return out

    return (np.ascontiguousarray(y_prompt), np.ascontiguousarray(y_sample),
            kout(0, 512, 8, 64), vout(0, 512, 4, 128), kout(512, 640, 2, 64), vout(512, 640, 2, 64),
            kout(640, 1152, 8, 64), vout(640, 1152, 8, 64))
```
